# Optimizing a Trainium2 kernel written in Bass

```python
import math
import jax
import jax.numpy as jnp
from jax import lax
import numpy as np


D_MODEL = 1024
BATCH = 2
SEQ = 8192
DEPTH = 2

HEAD_DIM = 64
CHUNK = 128
EPS = 1e-6

SSD_HEADS = 8
SSD_GROUPS = 2
SSD_STATE = 64
SSD_CONV = 4
SSD_WIDTH = SSD_HEADS * HEAD_DIM
SSD_XBC = SSD_WIDTH + 2 * SSD_GROUPS * SSD_STATE
SSD_COLS = SSD_WIDTH + SSD_XBC + SSD_HEADS
RET_HEADS = 8
RET_WIDTH = RET_HEADS * HEAD_DIM
RET_COLS = 4 * RET_WIDTH
EVEN_COLS = SSD_COLS + RET_COLS

RWKV_HEADS = 8
RWKV_WIDTH = RWKV_HEADS * HEAD_DIM
RWKV_DECAY_RANK = 64
RWKV_ICLR_RANK = 64
RWKV_GATE_RANK = 128
RWKV_COLS = 3 * RWKV_WIDTH + RWKV_DECAY_RANK + RWKV_ICLR_RANK + RWKV_GATE_RANK
RWKV_SPLITS = [RWKV_WIDTH, 2 * RWKV_WIDTH, 3 * RWKV_WIDTH,
               3 * RWKV_WIDTH + RWKV_DECAY_RANK,
               3 * RWKV_WIDTH + RWKV_DECAY_RANK + RWKV_ICLR_RANK]
RWKV_LN_EPS = 64e-5
RWKV_DECAY_SCALE = 0.606531
FOX_HEADS = 8
FOX_WIDTH = FOX_HEADS * HEAD_DIM
FOX_COLS = 3 * FOX_WIDTH + FOX_HEADS
ODD_COLS = RWKV_COLS + FOX_COLS

MIX_WIDTH = 1024
D_FF = 2816
FFN_CONV = 3
N_EVEN = (DEPTH + 1) // 2
N_ODD = DEPTH // 2

kernel_name = 'hybrid_ssd_retnet_rwkv7_fox_convffn'


def rms_norm(x, w, eps=EPS):
    xf = x.astype(jnp.float32)
    y = xf * lax.rsqrt(jnp.mean(xf * xf, axis=-1, keepdims=True) + eps)
    return (y * w.astype(jnp.float32)).astype(x.dtype)


def causal_dwconv(x, w, b):
    K = w.shape[0]
    S = x.shape[1]
    xp = jnp.pad(x, ((0, 0), (K - 1, 0), (0, 0)))
    out = b
    for k in range(K):
        out = out + w[k] * xp[:, k:k + S]
    return out


def rotary(x, pos):
    half = x.shape[-1] // 2
    inv = 1.0 / (10000.0 ** (jnp.arange(half, dtype=jnp.float32) / half))
    ang = pos.astype(jnp.float32)[:, None] * inv[None, :]
    cos = jnp.cos(ang)[None, :, None, :]
    sin = jnp.sin(ang)[None, :, None, :]
    x1, x2 = x[..., :half], x[..., half:]
    return jnp.concatenate([x1 * cos - x2 * sin, x1 * sin + x2 * cos], axis=-1)


def ssd_mixer(proj, conv_w, conv_b, dt_bias, a_log, d_skip, norm_w):
    b_, s_, _ = proj.shape
    nc = s_ // CHUNK
    r_ = SSD_HEADS // SSD_GROUPS
    z = proj[..., :SSD_WIDTH]
    xbc = jax.nn.silu(causal_dwconv(proj[..., SSD_WIDTH:SSD_WIDTH + SSD_XBC], conv_w, conv_b))
    dt = jax.nn.softplus(proj[..., SSD_WIDTH + SSD_XBC:] + dt_bias)
    x = xbc[..., :SSD_WIDTH].reshape(b_, nc, CHUNK, SSD_GROUPS, r_, HEAD_DIM)
    n_bc = SSD_GROUPS * SSD_STATE
    bm = xbc[..., SSD_WIDTH:SSD_WIDTH + n_bc].reshape(b_, nc, CHUNK, SSD_GROUPS, SSD_STATE)
    cm = xbc[..., SSD_WIDTH + n_bc:].reshape(b_, nc, CHUNK, SSD_GROUPS, SSD_STATE)
    dt = dt.reshape(b_, nc, CHUNK, SSD_GROUPS, r_)
    a = dt * (-jnp.exp(a_log)).reshape(SSD_GROUPS, r_)
    a_cum = jnp.cumsum(a, axis=2)
    xdt = x * dt[..., None]
    causal = jnp.tril(jnp.ones((CHUNK, CHUNK), dtype=bool))
    seg = a_cum[:, :, :, None] - a_cum[:, :, None, :]
    decay = jnp.exp(jnp.where(causal[:, :, None, None], seg, -jnp.inf))
    cb = jnp.einsum('bclgn,bcsgn->bclsg', cm, bm)
    y_diag = jnp.einsum('bclsg,bclsgr,bcsgrp->bclgrp', cb, decay, xdt)
    to_end = jnp.exp(a_cum[:, :, -1:] - a_cum)
    chunk_states = jnp.einsum('bclgn,bclgr,bclgrp->bcgrpn', bm, to_end, xdt)
    chunk_decay = jnp.exp(a_cum[:, :, -1])

    def step(h, inp):
        st, dc = inp
        return h * dc[..., None, None] + st, h

    _, states_in = lax.scan(step, jnp.zeros_like(chunk_states[:, 0]),
                            (jnp.moveaxis(chunk_states, 1, 0), jnp.moveaxis(chunk_decay, 1, 0)))
    states_in = jnp.moveaxis(states_in, 0, 1)
    y_off = jnp.einsum('bclgn,bcgrpn,bclgr->bclgrp', cm, states_in, jnp.exp(a_cum))
    y = y_diag + y_off + d_skip.reshape(SSD_GROUPS, r_)[:, :, None] * x
    y = y.reshape(b_, s_, SSD_WIDTH)
    return rms_norm(y * jax.nn.silu(z), norm_w)


def retention_mixer(proj, pos, norm_w):
    b_, s_, _ = proj.shape
    nc = s_ // CHUNK
    q, k, v, g = jnp.split(proj, 4, axis=-1)
    hs = (b_, s_, RET_HEADS, HEAD_DIM)
    cs = (b_, nc, CHUNK, RET_HEADS, HEAD_DIM)
    q = rotary(q.reshape(hs), pos).reshape(cs)
    k = (rotary(k.reshape(hs), pos) * HEAD_DIM ** -0.5).reshape(cs)
    v = v.reshape(cs)
    log_gamma = jnp.log1p(-(2.0 ** (-5.0 - jnp.arange(RET_HEADS, dtype=jnp.float32))))
    idx = jnp.arange(CHUNK, dtype=jnp.float32)
    rel = idx[:, None] - idx[None, :]
    intra = jnp.where(rel[None] >= 0,
                      jnp.exp(jnp.maximum(rel, 0.0)[None] * log_gamma[:, None, None]), 0.0)
    scores = jnp.einsum('bclhd,bcshd->bchls', q, k) * intra
    y_in = jnp.einsum('bchls,bcshe->bclhe', scores, v)
    k_to_end = jnp.exp((CHUNK - 1 - idx)[:, None] * log_gamma[None, :])
    chunk_kv = jnp.einsum('bcshd,sh,bcshe->bchde', k, k_to_end, v)
    chunk_gamma = jnp.exp(CHUNK * log_gamma)[:, None, None]

    def step(r, kv):
        return r * chunk_gamma + kv, r

    _, r_in = lax.scan(step, jnp.zeros_like(chunk_kv[:, 0]), jnp.moveaxis(chunk_kv, 1, 0))
    r_in = jnp.moveaxis(r_in, 0, 1)
    q_decay = jnp.exp((idx + 1.0)[:, None] * log_gamma[None, :])
    y_x = jnp.einsum('bclhd,bchde,lh->bclhe', q, r_in, q_decay)
    y = (y_in + y_x).reshape(hs)
    y = rms_norm(y, norm_w.reshape(RET_HEADS, HEAD_DIM)).reshape(b_, s_, RET_WIDTH)
    return jax.nn.silu(g) * y


def rwkv7_mixer(proj, mu, w0, w_up, a0, a_up, g_up, k_k, k_a, r_k, ln_w, ln_b):
    b_, s_, _ = proj.shape
    prev = jnp.pad(proj, ((0, 0), (1, 0), (0, 0)))[:, :-1]
    proj = proj + (prev - proj) * mu
    r, k, v, w_lo, a_lo, g_lo = jnp.split(proj, RWKV_SPLITS, axis=-1)
    w = jnp.exp(-RWKV_DECAY_SCALE * jax.nn.sigmoid(w0 + jnp.tanh(w_lo) @ w_up))
    a = jax.nn.sigmoid(a0 + a_lo @ a_up)
    g = jax.nn.sigmoid(g_lo) @ g_up
    hs = (b_, s_, RWKV_HEADS, HEAD_DIM)
    r, k, v, w, a = (t.reshape(hs) for t in (r, k, v, w, a))
    kk = k * k_k.reshape(RWKV_HEADS, HEAD_DIM)
    kk = kk / jnp.maximum(jnp.sqrt(jnp.sum(kk * kk, axis=-1, keepdims=True)), 1e-12)
    k = k * (1.0 + (a - 1.0) * k_a.reshape(RWKV_HEADS, HEAD_DIM))

    def step(state, inp):
        r_t, w_t, k_t, v_t, kk_t, a_t = inp
        sa = jnp.einsum('bhvk,bhk->bhv', state, -kk_t)
        state = (state * w_t[:, :, None, :] + sa[..., None] * (kk_t * a_t)[:, :, None, :]
                 + v_t[..., None] * k_t[:, :, None, :])
        return state, jnp.einsum('bhvk,bhk->bhv', state, r_t)

    state0 = jnp.zeros((b_, RWKV_HEADS, HEAD_DIM, HEAD_DIM), dtype=r.dtype)
    seq_first = lambda t: jnp.moveaxis(t, 1, 0)
    _, y = lax.scan(step, state0, (seq_first(r), seq_first(w), seq_first(k),
                                   seq_first(v), seq_first(kk), seq_first(a)))
    y = jnp.moveaxis(y, 0, 1)
    mean = jnp.mean(y, axis=-1, keepdims=True)
    var = jnp.mean(jnp.square(y - mean), axis=-1, keepdims=True)
    y = ((y - mean) * lax.rsqrt(var + RWKV_LN_EPS) * ln_w.reshape(RWKV_HEADS, HEAD_DIM)
         + ln_b.reshape(RWKV_HEADS, HEAD_DIM))
    bonus = jnp.sum(r * k * r_k, axis=-1, keepdims=True) * v
    return (y + bonus).reshape(b_, s_, RWKV_WIDTH) * g


def fox_mixer(proj, q_norm_w, k_norm_w, f_bias):
    b_, s_, _ = proj.shape
    nb = s_ // CHUNK
    q, k, v, f = jnp.split(proj, [FOX_WIDTH, 2 * FOX_WIDTH, 3 * FOX_WIDTH], axis=-1)
    hs = (b_, s_, FOX_HEADS, HEAD_DIM)
    q = rms_norm(q.reshape(hs), q_norm_w) * HEAD_DIM ** -0.5
    k = rms_norm(k.reshape(hs), k_norm_w)
    v = v.reshape(hs)
    log_f = jax.nn.log_sigmoid((f + f_bias).astype(jnp.float32))
    c = jnp.cumsum(log_f, axis=1)
    c_hs = jnp.moveaxis(c, -1, 1)
    qb = jnp.moveaxis(q.reshape(b_, nb, CHUNK, FOX_HEADS, HEAD_DIM), 1, 0)
    cb = jnp.moveaxis(c.reshape(b_, nb, CHUNK, FOX_HEADS), 1, 0)
    k_pos = jnp.arange(s_)

    def block(args):
        i, q_i, c_i = args
        logits = jnp.einsum('bqhd,bkhd->bhqk', q_i, k).astype(jnp.float32)
        logits = logits + jnp.moveaxis(c_i, -1, 1)[..., None] - c_hs[:, :, None, :]
        q_pos = i * CHUNK + jnp.arange(CHUNK)
        logits = jnp.where(q_pos[:, None] >= k_pos[None, :], logits, -jnp.inf)
        p = jax.nn.softmax(logits, axis=-1)
        return jnp.einsum('bhqk,bkhd->bqhd', p.astype(v.dtype), v)

    out = lax.map(block, (jnp.arange(nb), qb, cb))
    return jnp.moveaxis(out, 0, 1).reshape(b_, s_, FOX_WIDTH)


def even_layer(x, pos, norm_w, w_in, conv_w, conv_b, dt_bias, a_log, d_skip, ssd_norm_w, ret_norm_w, w_out):
    h = (rms_norm(x, norm_w) @ w_in).astype(jnp.float32)
    y_ssd = ssd_mixer(h[..., :SSD_COLS], conv_w, conv_b, dt_bias, a_log, d_skip, ssd_norm_w)
    y_ret = retention_mixer(h[..., SSD_COLS:], pos, ret_norm_w)
    y = jnp.concatenate([y_ssd, y_ret], axis=-1).astype(x.dtype)
    return x + y @ w_out


def odd_layer(x, norm_w, w_in, mu, w0, w_up, a0, a_up, g_up, k_k, k_a, r_k, ln_w, ln_b,
              q_norm_w, k_norm_w, f_bias, w_out):
    h = (rms_norm(x, norm_w) @ w_in).astype(jnp.float32)
    y_rwkv = rwkv7_mixer(h[..., :RWKV_COLS], mu, w0, w_up, a0, a_up, g_up, k_k, k_a, r_k, ln_w, ln_b)
    y_fox = fox_mixer(h[..., RWKV_COLS:], q_norm_w, k_norm_w, f_bias)
    y = jnp.concatenate([y_rwkv, y_fox], axis=-1).astype(x.dtype)
    return x + y @ w_out


def conv_ffn(x, w_up, conv_w, conv_b, w_down):
    gate, up = jnp.split(x @ w_up, 2, axis=-1)
    gate = causal_dwconv(gate, conv_w, conv_b)
    return (jax.nn.silu(gate) * up) @ w_down


def setup_inputs(seed: int = 0) -> dict:
    key = jax.random.key(seed)
    ks = iter(jax.random.split(key, 48))
    f32 = jnp.float32

    def nrm(shape, scale):
        return jax.random.normal(next(ks), shape, f32) * scale

    def uni(shape, lo, hi):
        return jax.random.uniform(next(ks), shape, f32, minval=lo, maxval=hi)

    dt0 = jnp.exp(uni((N_EVEN, SSD_HEADS), math.log(1e-3), math.log(1e-1)))
    return {
        'x': nrm((BATCH, SEQ, D_MODEL), 1.0),
        'ev_norm_w': 1.0 + nrm((N_EVEN, D_MODEL), 0.02),
        'ev_w_in': nrm((N_EVEN, D_MODEL, EVEN_COLS), D_MODEL ** -0.5),
        'ev_ssd_conv_w': nrm((N_EVEN, SSD_CONV, SSD_XBC), SSD_CONV ** -0.5),
        'ev_ssd_conv_b': nrm((N_EVEN, SSD_XBC), 0.02),
        'ev_ssd_dt_bias': dt0 + jnp.log(-jnp.expm1(-dt0)),
        'ev_ssd_a_log': jnp.log(uni((N_EVEN, SSD_HEADS), 1.0, 16.0)),
        'ev_ssd_d': 1.0 + nrm((N_EVEN, SSD_HEADS), 0.1),
        'ev_ssd_norm_w': 1.0 + nrm((N_EVEN, SSD_WIDTH), 0.02),
        'ev_ret_norm_w': 1.0 + nrm((N_EVEN, RET_WIDTH), 0.02),
        'ev_w_out': nrm((N_EVEN, MIX_WIDTH, D_MODEL), MIX_WIDTH ** -0.5),
        'od_norm_w': 1.0 + nrm((N_ODD, D_MODEL), 0.02),
        'od_w_in': nrm((N_ODD, D_MODEL, ODD_COLS), D_MODEL ** -0.5),
        'od_rwkv_mu': uni((N_ODD, RWKV_COLS), 0.0, 1.0),
        'od_rwkv_w0': uni((N_ODD, RWKV_WIDTH), -4.0, 2.0),
        'od_rwkv_w_up': nrm((N_ODD, RWKV_DECAY_RANK, RWKV_WIDTH), 0.1),
        'od_rwkv_a0': nrm((N_ODD, RWKV_WIDTH), 0.5),
        'od_rwkv_a_up': nrm((N_ODD, RWKV_ICLR_RANK, RWKV_WIDTH), 0.1),
        'od_rwkv_g_up': nrm((N_ODD, RWKV_GATE_RANK, RWKV_WIDTH), RWKV_GATE_RANK ** -0.5),
        'od_rwkv_k_k': 0.85 + nrm((N_ODD, RWKV_WIDTH), 0.05),
        'od_rwkv_k_a': 1.0 + nrm((N_ODD, RWKV_WIDTH), 0.05),
        'od_rwkv_r_k': nrm((N_ODD, RWKV_HEADS, HEAD_DIM), 0.1),
        'od_rwkv_ln_w': 1.0 + nrm((N_ODD, RWKV_WIDTH), 0.02),
        'od_rwkv_ln_b': nrm((N_ODD, RWKV_WIDTH), 0.02),
        'od_fox_q_norm_w': 1.0 + nrm((N_ODD, HEAD_DIM), 0.02),
        'od_fox_k_norm_w': 1.0 + nrm((N_ODD, HEAD_DIM), 0.02),
        'od_fox_f_bias': uni((N_ODD, FOX_HEADS), 1.0, 5.0),
        'od_w_out': nrm((N_ODD, MIX_WIDTH, D_MODEL), MIX_WIDTH ** -0.5),
        'ffn_norm_w': 1.0 + nrm((DEPTH, D_MODEL), 0.02),
        'ffn_w_up': nrm((DEPTH, D_MODEL, 2 * D_FF), D_MODEL ** -0.5),
        'ffn_conv_w': nrm((DEPTH, FFN_CONV, D_FF), FFN_CONV ** -0.5),
        'ffn_conv_b': nrm((DEPTH, D_FF), 0.02),
        'ffn_w_down': nrm((DEPTH, D_FF, D_MODEL), D_FF ** -0.5),
    }


def reference(x, ev_norm_w, ev_w_in, ev_ssd_conv_w, ev_ssd_conv_b, ev_ssd_dt_bias, ev_ssd_a_log,
              ev_ssd_d, ev_ssd_norm_w, ev_ret_norm_w, ev_w_out,
              od_norm_w, od_w_in, od_rwkv_mu, od_rwkv_w0, od_rwkv_w_up, od_rwkv_a0, od_rwkv_a_up,
              od_rwkv_g_up, od_rwkv_k_k, od_rwkv_k_a, od_rwkv_r_k, od_rwkv_ln_w, od_rwkv_ln_b,
              od_fox_q_norm_w, od_fox_k_norm_w, od_fox_f_bias, od_w_out,
              ffn_norm_w, ffn_w_up, ffn_conv_w, ffn_conv_b, ffn_w_down):
    pos = jnp.arange(x.shape[1])
    for layer in range(DEPTH):
        i = layer // 2
        if layer % 2 == 0:
            x = even_layer(x, pos, ev_norm_w[i], ev_w_in[i], ev_ssd_conv_w[i], ev_ssd_conv_b[i],
                           ev_ssd_dt_bias[i], ev_ssd_a_log[i], ev_ssd_d[i], ev_ssd_norm_w[i],
                           ev_ret_norm_w[i], ev_w_out[i])
        else:
            x = odd_layer(x, od_norm_w[i], od_w_in[i], od_rwkv_mu[i], od_rwkv_w0[i], od_rwkv_w_up[i],
                          od_rwkv_a0[i], od_rwkv_a_up[i], od_rwkv_g_up[i], od_rwkv_k_k[i],
                          od_rwkv_k_a[i], od_rwkv_r_k[i], od_rwkv_ln_w[i], od_rwkv_ln_b[i],
                          od_fox_q_norm_w[i], od_fox_k_norm_w[i], od_fox_f_bias[i], od_w_out[i])
        x = x + conv_ffn(rms_norm(x, ffn_norm_w[layer]), ffn_w_up[layer], ffn_conv_w[layer],
                         ffn_conv_b[layer], ffn_w_down[layer])
    return x
```

```python
from contextlib import ExitStack
import numpy as np
import concourse.bass as bass
import concourse.mybir as mybir

F32 = mybir.dt.float32
BF16 = mybir.dt.bfloat16
AF = mybir.ActivationFunctionType
ALU = mybir.AluOpType
AX = mybir.AxisListType

ENGS = ("pe", "act", "dve", "pool", "sp")


class Trk:
    __slots__ = ("lw", "rd", "name", "dsem", "dcnt", "excl")

    def __init__(self, name=""):
        self.excl = False
        self.lw = None
        self.rd = {}
        self.name = name
        self.dsem = None
        self.dcnt = 0


class V:
    __slots__ = ("ap", "trks", "dram")

    def __init__(self, ap, trks, dram=False):
        self.ap = ap
        self.trks = trks
        self.dram = dram


class T:
    def __init__(self, handle, name, ntrk=1, dram=False):
        self.h = handle
        self.name = name
        self.dram = dram
        self.trks = [Trk(f"{name}.{i}") for i in range(ntrk)]

    def __getitem__(self, key):
        return V(self.h[key], self.trks, self.dram)

    def v(self, key, tids):
        return V(self.h[key], [self.trks[i] for i in tids], self.dram)

    def whole(self, ap):
        return V(ap, self.trks, self.dram)


class Prog:
    def __init__(self, nc, same_engine_sync=True):
        self.nc = nc
        self.es = ExitStack()
        self.ops = {e: [] for e in ENGS}
        self.cnt = {e: 0 for e in ENGS}
        self.seen = {e: {} for e in ENGS}
        self.same = same_engine_sync
        self.n_inst = 0
        self.sems = {e: self.es.enter_context(nc.semaphore(f"s_{e}")) for e in ENGS}
        self.n_wait = 0

    def sb(self, name, shape, dtype=F32, ntrk=1):
        h = self.es.enter_context(self.nc.sbuf_tensor(name, list(shape), dtype))
        return T(h, name, ntrk)

    def ps(self, name, shape, dtype=F32, ntrk=1):
        h = self.es.enter_context(self.nc.psum_tensor(name, list(shape), dtype))
        t = T(h, name, ntrk)
        for k in t.trks:
            k.excl = True
        return t

    def dram(self, name, shape, dtype=F32, kind="Internal", ntrk=1):
        h = self.nc.dram_tensor(name, list(shape), dtype, kind=kind)
        return T(h.ap(), name, ntrk, dram=True)

    def _semof(self, key):
        return self.sems[key] if isinstance(key, str) else key.dsem

    def op(self, eng, fn, reads=(), writes=(), is_dma=False):
        waits = {}
        seen = self.seen[eng]

        def need(dep):
            if dep is None:
                return
            f, idx = dep
            if f == eng and (eng == "pe" or not self.same):
                return
            if seen.get(f, 0) >= idx:
                return
            if waits.get(f, 0) < idx:
                waits[f] = idx

        for v in reads:
            for t in v.trks:
                need(t.lw)
                if t.excl:
                    for f, idx in t.rd.items():
                        if f != eng:
                            need((f, idx))
        for v in writes:
            for t in v.trks:
                need(t.lw)
                for f, idx in t.rd.items():
                    need((f, idx))
        for f, idx in waits.items():
            seen[f] = idx
        if is_dma:
            assert len(writes) == 1 and len(writes[0].trks) == 1, "DMA must write exactly one tracker"
            t = writes[0].trks[0]
            if getattr(writes[0], "dram", False) and len(reads) == 1 and len(reads[0].trks) == 1 and not getattr(reads[0], "dram", False):
                t = reads[0].trks[0]
            if t.dsem is None:
                t.dsem = self.es.enter_context(self.nc.semaphore("d_" + t.name.replace(".", "_")))
            t.dcnt += 16
            key, me = t, t.dcnt
            inc = (t.dsem, 16)
            self.dma_owners = getattr(self, "dma_owners", {})
            self.dma_owners[id(t)] = t
        else:
            self.cnt[eng] += 1
            key, me = eng, self.cnt[eng]
            inc = (self.sems[eng], 1)
        self.ops[eng].append((list(waits.items()), fn, inc, me))
        self.n_wait += len(waits)
        for v in reads:
            for t in v.trks:
                t.rd[key] = me
        for v in writes:
            for t in v.trks:
                t.lw = (key, me)
                t.rd = {}
        self.n_inst += 1
        return me

    def mm(self, out, lhsT, rhs, start=True, stop=True, **kw):
        self.op("pe", lambda e: e.matmul(out.ap, lhsT.ap, rhs.ap, start=start, stop=stop, **kw),
                reads=[lhsT, rhs] + ([] if start else [out]), writes=[out])

    def tr(self, out, in_, ident):
        self.op("pe", lambda e: e.transpose(out.ap, in_.ap, ident.ap), reads=[in_, ident], writes=[out])

    def act(self, out, in_, func, bias=None, scale=None, accum_out=None, eng="act"):
        kw = {}
        rd = [in_]
        wr = [out]
        if bias is not None:
            if isinstance(bias, V):
                kw["bias"] = bias.ap; rd.append(bias)
            else:
                kw["bias"] = bias
        if scale is not None:
            if isinstance(scale, V):
                kw["scale"] = scale.ap; rd.append(scale)
            else:
                kw["scale"] = scale
        if accum_out is not None:
            kw["accum_out"] = accum_out.ap; wr.append(accum_out)
        self.op(eng, lambda e: e.activation(out.ap, in_.ap, func, **kw), reads=rd, writes=wr)

    def tt(self, out, in0, in1, op, eng="dve"):
        self.op(eng, lambda e: e.tensor_tensor(out.ap, in0.ap, in1.ap, op), reads=[in0, in1], writes=[out])

    def ts(self, out, in0, s1, op0, s2=None, op1=None, eng="dve", accum_out=None):
        rd = [in0]
        a1 = s1.ap if isinstance(s1, V) else s1
        a2 = s2.ap if isinstance(s2, V) else s2
        if isinstance(s1, V): rd.append(s1)
        if isinstance(s2, V): rd.append(s2)
        wr = [out]
        kw = {}
        if accum_out is not None:
            kw["accum_out"] = accum_out.ap; wr.append(accum_out)
        if op1 is None:
            self.op(eng, lambda e: e.tensor_scalar(out.ap, in0.ap, a1, None, op0, **kw), reads=rd, writes=wr)
        else:
            self.op(eng, lambda e: e.tensor_scalar(out.ap, in0.ap, a1, a2, op0, op1, **kw), reads=rd, writes=wr)

    def stt(self, out, in0, s, in1, op0, op1, eng="dve"):
        rd = [in0, in1]
        a = s.ap if isinstance(s, V) else s
        if isinstance(s, V): rd.append(s)
        self.op(eng, lambda e: e.scalar_tensor_tensor(out.ap, in0.ap, a, in1.ap, op0, op1), reads=rd, writes=[out])

    def copy(self, out, in_, eng="dve"):
        if eng == "act":
            self.op(eng, lambda e: e.copy(out.ap, in_.ap), reads=[in_], writes=[out])
        else:
            self.op(eng, lambda e: e.tensor_copy(out.ap, in_.ap), reads=[in_], writes=[out])

    def recip(self, out, in_):
        self.op("dve", lambda e: e.reciprocal(out.ap, in_.ap), reads=[in_], writes=[out])

    def memset(self, out, val, eng="dve"):
        self.op(eng, lambda e: e.memset(out.ap, val), writes=[out])

    def dma(self, out, in_, eng="sp", **kw):
        self.op(eng, lambda e: e.dma_start(out.ap, in_.ap, **kw), reads=[in_], writes=[out], is_dma=True)

    def dbg(self, name, view, shape):
        if not getattr(self, "debug", False):
            return
        d = self.dram("dbg_" + name, list(shape), F32, kind="ExternalOutput")
        idx = tuple(slice(None) for _ in shape)
        self.dma(d[idx], view)
        self.dbg_outs = getattr(self, "dbg_outs", []) + [d]

    def finish(self, outs, eng="sp"):
        waits = [(t, t.dcnt) for t in getattr(self, "dma_owners", {}).values()]
        self.ops[eng].append((waits, None, None, None))

    def emit(self):
        nc = self.nc
        import bisect
        ref = {e: set() for e in ENGS}
        for e in ENGS:
            for waits, fn, inc, me in self.ops[e]:
                for f, idx in waits:
                    if isinstance(f, str):
                        ref[f].add(idx)
        ranks = {e: sorted(ref[e]) for e in ENGS}
        self.n_inc = sum(len(v) for v in ranks.values())

        def val(f, idx):
            if isinstance(f, str):
                return bisect.bisect_left(ranks[f], idx) + 1 + self.base[f]
            return idx

        if not hasattr(self, "base"):
            self.base = {e: 0 for e in ENGS}
        with nc.Block() as block:
            def mk(ename):
                def body(e):
                    rs = ref[ename]
                    for waits, fn, inc, me in self.ops[ename]:
                        for f, idx in waits:
                            e.wait_ge(self._semof(f), val(f, idx))
                        if fn is not None:
                            ins = fn(e)
                            if inc[1] == 16:
                                ins.then_inc(inc[0], 16)
                            elif me in rs:
                                ins.then_inc(inc[0], 1)
                return body
            block.tensor(mk("pe"))
            block.scalar(mk("act"))
            block.vector(mk("dve"))
            block.gpsimd(mk("pool"))
            block.sync(mk("sp"))
        self.es.close()


D = 1024
DFF = 2816
NFC = DFF // 128
SEQ = 8192
EPS = 1e-6


def _cast_weight(P, dst_bf, src_ap_fn, nk, ncols, stage, engs=("act", "pool", "dve")):
    CW = stage[0].h.shape[1]
    i = 0
    for k in range(nk):
        for c0 in range(0, ncols, CW):
            cw = min(CW, ncols - c0)
            st = stage[i % len(stage)]
            P.dma(st[:, 0:cw], src_ap_fn(k, c0, cw))
            P.copy(dst_bf[:, k, c0:c0 + cw], st[:, 0:cw], eng=engs[i % len(engs)])
            i += 1


def build_phaseB(even, NT=2048, TS=256):
    nc = bass.Bass("TRN2", target_bir_lowering=False)
    P = Prog(nc)
    NC = NT + 2
    xT = P.dram("xT", [D, NC], F32, kind="ExternalInput")
    yT = P.dram("yT", [D, NC], F32, kind="ExternalInput")
    w_out = P.dram("w_out", [D, D], F32, kind="ExternalInput")
    w_up = P.dram("w_up", [D, 2 * DFF], F32, kind="ExternalInput")
    w_down = P.dram("w_down", [DFF, D], F32, kind="ExternalInput")
    vec_d = P.dram("vec_d", [128, 12], F32, kind="ExternalInput")
    cv_d = P.dram("cv_d", [128, NFC * 4], F32, kind="ExternalInput")
    oT = P.dram("oT", [D, NT], F32, kind="ExternalOutput", ntrk=1)

    wout_bf = P.sb("wout_bf", [128, 8, D], BF16)
    wup_bf = P.sb("wup_bf", [128, 8, 2 * DFF], BF16)
    wdn_bf = P.sb("wdn_bf", [128, NFC, D], BF16)
    stage = [P.sb(f"stage{i}", [128, 512], F32) for i in range(3)]
    vec = P.sb("vec", [128, 12])
    cv = P.sb("cv", [128, NFC * 4])
    ones = P.sb("ones", [128, 128])
    hist = P.sb("hist", [128, NFC, 2])
    xs = [P.sb(f"xs{i}", [128, 8, TS]) for i in range(2)]
    ys = [P.sb(f"ys{i}", [128, 8, TS]) for i in range(1)]
    ybf = P.sb("ybf", [128, 8, TS], BF16)
    xn = P.sb("xn", [128, 8, TS], BF16)
    hb = P.sb("hb", [128, NFC, TS], BF16)
    sq = [P.sb(f"sq{i}", [128, TS]) for i in range(2)]
    rstd = P.sb("rstd", [128, TS])
    gbuf = [P.sb(f"gbuf{i}", [128, TS + 2]) for i in range(2)]
    acc = [P.sb(f"acc{i}", [128, TS]) for i in range(2)]
    sg = [P.sb(f"sg{i}", [128, TS]) for i in range(2)]
    pa = [P.ps(f"pa{i}", [128, 512]) for i in range(3)]
    pb = [P.ps(f"pb{i}", [128, 512]) for i in range(3)]
    pst = P.ps("pst", [128, 512])

    P.dma(vec[:, :], vec_d[:, :])
    P.dma(cv[:, :], cv_d[:, :])
    P.memset(ones[:, :], 1.0)
    P.memset(hist[:, :, :], 0.0)
    _cast_weight(P, wout_bf, lambda k, c0, cw: w_out[k * 128:(k + 1) * 128, c0:c0 + cw], 8, D, stage)
    _cast_weight(P, wup_bf, lambda k, c0, cw: w_up[k * 128:(k + 1) * 128, c0:c0 + cw], 8, 2 * DFF, stage)
    _cast_weight(P, wdn_bf, lambda k, c0, cw: w_down[k * 128:(k + 1) * 128, c0:c0 + cw], NFC, D, stage)

    xT3 = V(xT.h.rearrange("(k p) t -> p k t", p=128), xT.trks, True)
    yT3 = V(yT.h.rearrange("(k p) t -> p k t", p=128), yT.trks, True)
    oT3 = oT.h.rearrange("(k p) t -> p k t", p=128)

    chunks = [(0, 2, True)] + [(2 + i * TS, 2 + (i + 1) * TS, False) for i in range(NT // TS)]
    rot = {"pa": 0, "pb": 0, "g": 0, "sq": 0}

    def nxt(key, lst):
        r = lst[rot[key] % len(lst)]
        rot[key] += 1
        return r

    def rms_stats(src, nk, ncols_total, T):
        for k in range(nk):
            s = nxt("sq", sq)
            P.act(s[:, :T], src[:, k, :T], AF.Square)
            P.mm(pst[:, :T], ones[:, :], s[:, :T], start=(k == 0), stop=(k == nk - 1))
        P.act(rstd[:, :T], pst[:, :T], AF.Sqrt, bias=EPS, scale=1.0 / ncols_total)
        P.recip(rstd[:, :T], rstd[:, :T])

    for ci, (c0, c1, halo) in enumerate(chunks):
        T = c1 - c0
        x = xs[ci % 2]
        y = ys[0]
        P.dma(x[:, :, :T], V(xT3.ap[:, :, c0:c1], xT.trks, True))
        P.dma(y[:, :, :T], V(yT3.ap[:, :, c0:c1], yT.trks, True), eng="act")
        if even:
            rms_stats(y, 4, 512, T)
            for k in range(4):
                P.stt(ybf[:, k, :T], y[:, k, :T], vec[:, 8 + k:9 + k], rstd[:, :T], ALU.mult, ALU.mult)
            for k in range(4, 8):
                P.copy(ybf[:, k, :T], y[:, k, :T], eng="pool")
        else:
            for k in range(8):
                P.copy(ybf[:, k, :T], y[:, k, :T], eng=("pool" if k % 2 else "dve"))
        for dc in range(8):
            ps = nxt("pa", pa)
            for mc in range(8):
                P.mm(ps[:, :T], wout_bf[:, mc, dc * 128:(dc + 1) * 128], ybf[:, mc, :T],
                     start=(mc == 0), stop=(mc == 7))
            P.tt(x[:, dc, :T], x[:, dc, :T], ps[:, :T], ALU.add)
        rms_stats(x, 8, D, T)
        for k in range(8):
            P.stt(xn[:, k, :T], x[:, k, :T], vec[:, k:k + 1], rstd[:, :T], ALU.mult, ALU.mult)
        for fc in range(NFC):
            pg = nxt("pa", pa)
            for k in range(8):
                P.mm(pg[:, :T], wup_bf[:, k, fc * 128:(fc + 1) * 128], xn[:, k, :T],
                     start=(k == 0), stop=(k == 7))
            if halo:
                P.copy(hist[:, fc, :], pg[:, 0:2], eng="act")
                continue
            pu = nxt("pb", pb)
            for k in range(8):
                P.mm(pu[:, :T], wup_bf[:, k, DFF + fc * 128:DFF + (fc + 1) * 128], xn[:, k, :T],
                     start=(k == 0), stop=(k == 7))
            g = nxt("g", gbuf)
            a = acc[fc % 2]
            s = sg[fc % 2]
            P.copy(g[:, 0:2], hist[:, fc, :], eng="pool")
            P.copy(g[:, 2:2 + T], pg[:, :T], eng="act")
            P.copy(hist[:, fc, :], g[:, T:T + 2], eng="pool")
            cb = fc * 4
            P.ts(a[:, :T], g[:, 2:2 + T], cv[:, cb + 2:cb + 3], ALU.mult, cv[:, cb + 3:cb + 4], ALU.add)
            P.stt(a[:, :T], g[:, 1:1 + T], cv[:, cb + 1:cb + 2], a[:, :T], ALU.mult, ALU.add)
            P.stt(a[:, :T], g[:, 0:T], cv[:, cb:cb + 1], a[:, :T], ALU.mult, ALU.add)
            P.act(s[:, :T], a[:, :T], AF.Silu)
            P.tt(hb[:, fc, :T], s[:, :T], pu[:, :T], ALU.mult)
        if halo:
            continue
        for dc in range(8):
            ps = nxt("pa", pa)
            for fc in range(NFC):
                P.mm(ps[:, :T], wdn_bf[:, fc, dc * 128:(dc + 1) * 128], hb[:, fc, :T],
                     start=(fc == 0), stop=(fc == NFC - 1))
            P.tt(x[:, dc, :T], x[:, dc, :T], ps[:, :T], ALU.add)
        P.dma(V(oT3[:, :, c0 - 2:c1 - 2], oT.trks, True), x[:, :, :T])
    P.finish([oT])
    P.emit()
    return nc, P


class Banks:
    def __init__(self, P):
        self.b = [P.ps(f"bank{i}", [128, 512]) for i in range(8)]
        self.i = 0

    def __call__(self):
        r = self.b[self.i % 8]
        self.i += 1
        return r


def _load_consts(P, names_shapes):
    out = {}
    for name, shape in names_shapes:
        d = P.dram(name, shape, F32, kind="ExternalInput")
        s = P.sb(name + "_sb", shape)
        idx = tuple(slice(None) for _ in shape)
        P.dma(s[idx], d[idx])
        out[name] = s
    return out


def _rms_xn(P, bank, x, xn, ones, sq, rstd, colv, T):
    pst = bank()
    for k in range(8):
        s = sq[k % 2]
        P.act(s[:, :T], x[:, k, :T], AF.Square)
        P.mm(pst[:, :T], ones[:, :], s[:, :T], start=(k == 0), stop=(k == 7))
    P.act(rstd[:, :T], pst[:, :T], AF.Sqrt, bias=EPS, scale=1.0 / D)
    P.recip(rstd[:, :T], rstd[:, :T])
    for k in range(8):
        P.stt(xn[:, k, :T], x[:, k, :T], colv[:, k:k + 1], rstd[:, :T], ALU.mult, ALU.mult)


def _conv_silu(P, out, buf, rows, T, colv, c0, ntap, acc):
    r = slice(0, rows)
    K = ntap
    P.ts(acc[r, :T], buf[r, K - 1:K - 1 + T], colv[r, c0 + K - 1:c0 + K], ALU.mult, colv[r, c0 + K:c0 + K + 1], ALU.add)
    for k in range(K - 2, -1, -1):
        P.stt(acc[r, :T], buf[r, k:k + T], colv[r, c0 + k:c0 + k + 1], acc[r, :T], ALU.mult, ALU.add)
    P.act(out[r, :T], acc[r, :T], AF.Silu)
    P.copy(buf[r, 0:K - 1], buf[r, T:T + K - 1], eng="pool")


def build_phaseA0(S=SEQ, TS=512, debug=False):
    nc = bass.Bass("TRN2", target_bir_lowering=False)
    P = Prog(nc)
    P.debug = debug
    bank = Banks(P)
    NSC = S // TS
    xT = P.dram("xT", [D, S], F32, kind="ExternalInput")
    wfm = P.dram("wfm", [D, 1024], F32, kind="ExternalInput")
    wtm = P.dram("wtm", [D, 130], F32, kind="ExternalInput")
    cosT = P.dram("cosT", [64, S], F32, kind="ExternalInput")
    sinT = P.dram("sinT", [64, S], F32, kind="ExternalInput")
    rowv = P.dram("rowv", [1, 4], F32, kind="ExternalInput")
    yT = P.dram("yT", [256, S], F32, kind="ExternalOutput")
    C = _load_consts(P, [("colv", [128, 27]), ("dtab", [128, 2, 128]), ("qkdec", [128, 4]),
                         ("ident", [128, 128]), ("tri", [128, 128]), ("maskb", [128, 128])])
    colv, dtab, qkdec, ident, tri, maskb = (C[k] for k in ("colv", "dtab", "qkdec", "ident", "tri", "maskb"))
    rowb = P.sb("rowb", [128, 4])
    P.dma(rowb[:, :], V(rowv.h[0:1, :].partition_broadcast(128), rowv.trks, True))
    nA = P.sb("nA", [128, 2])
    P.act(nA[:, :], rowb[:, 2:4], AF.Exp)
    P.ts(nA[:, :], nA[:, :], -1.0, ALU.mult)
    ones = P.sb("ones", [128, 128])
    P.memset(ones[:, :], 1.0)

    wfm_bf = P.sb("wfm_bf", [128, 8, 1024], BF16)
    wtm_bf = P.sb("wtm_bf", [128, 8, 130], BF16)
    stage = [P.sb(f"stage{i}", [128, 512], F32) for i in range(3)]
    _cast_weight(P, wfm_bf, lambda k, c0, cw: wfm[k * 128:(k + 1) * 128, c0:c0 + cw], 8, 1024, stage)
    _cast_weight(P, wtm_bf, lambda k, c0, cw: wtm[k * 128:(k + 1) * 128, c0:c0 + cw], 8, 130, stage)

    xs_ = [P.sb(f"xs{i}", [128, 8, TS]) for i in range(2)]
    xn = P.sb("xn", [128, 8, TS], BF16)
    sq = [P.sb(f"sq{i}", [128, TS]) for i in range(2)]
    rstd = P.sb("rstd", [128, TS])
    cosb = [P.sb(f"cosb{i}", [64, TS]) for i in range(2)]
    sinb = [P.sb(f"sinb{i}", [64, TS]) for i in range(2)]
    zsil = P.sb("zsil", [128, TS])
    gsil = P.sb("gsil", [128, TS])
    xbuf = P.sb("xbuf", [128, TS + 3])
    bbuf = P.sb("bbuf", [64, TS + 3])
    cbuf = P.sb("cbuf", [64, TS + 3])
    for b_ in (xbuf, bbuf, cbuf):
        P.memset(b_[:, 0:3], 0.0)
    cacc = P.sb("cacc", [128, TS])
    xc = P.sb("xc", [128, TS])
    Bc = P.sb("Bc", [64, TS])
    Cc = P.sb("Cc", [64, TS])
    qrot = [P.sb(f"qrot{h}", [64, TS]) for h in range(2)]
    krot = [P.sb(f"krot{h}", [64, TS]) for h in range(2)]
    rt1 = P.sb("rt1", [64, TS])
    rt2 = P.sb("rt2", [64, TS])
    gout = [P.sb(f"gout{i}", [128, TS]) for i in range(2)]
    rout = [P.sb(f"rout{i}", [128, TS]) for i in range(2)]
    St = [P.sb(f"St{h}", [64, 64]) for h in range(2)]
    R = [P.sb(f"R{h}", [64, 64]) for h in range(2)]
    for t_ in St + R:
        P.memset(t_[:, :], 0.0)
    vtok = P.sb("vtok", [128, 128]); vk = P.sb("vk", [128, 128])
    ex = P.sb("ex", [128, 2]); dt = P.sb("dt", [128, 2]); a_ = P.sb("a_", [128, 2])
    nacum = P.sb("nacum", [128, 2]); eac = P.sb("eac", [128, 2]); tot = P.sb("tot", [128, 2])
    toend = P.sb("toend", [128, 2]); cd = P.sb("cd", [128, 2])
    xdt = P.sb("xdt", [128, 128]); xte = P.sb("xte", [128, 128]); btok = P.sb("btok", [128, 64])
    atri = [P.sb(f"atri{h}", [128, 128]) for h in range(2)]
    dT = [P.sb(f"dT{h}", [128, 128]) for h in range(2)]
    mT = [P.sb(f"mT{h}", [128, 128]) for h in range(2)]
    mTr = [P.sb(f"mTr{h}", [128, 128]) for h in range(2)]
    tmpy = [P.sb(f"tmpy{h}", [128, 64]) for h in range(2)]
    Y = P.sb("Y", [128, 128]); Yr = P.sb("Yr", [128, 128]); Yn = P.sb("Yn", [128, 128])
    ktok = [P.sb(f"ktok{h}", [128, 64]) for h in range(2)]
    junk = P.sb("junk", [128, 64]); ssr = P.sb("ssr", [128, 2]); rs = P.sb("rs", [128, 2])
    go = P.sb("go", [128, 128])

    xT3 = xT.h.rearrange("(k p) t -> p k t", p=128)

    def proj_fm(c0, M, T):
        ps = bank()
        for k in range(8):
            P.mm(ps[0:M, :T], wfm_bf[:, k, c0:c0 + M], xn[:, k, :T], start=(k == 0), stop=(k == 7))
        return ps

    for sc in range(NSC):
        t0 = sc * TS
        T = TS
        x = xs_[sc % 2]
        cb_, sb_ = cosb[sc % 2], sinb[sc % 2]
        P.dma(x[:, :, :], V(xT3[:, :, t0:t0 + T], xT.trks, True))
        P.dma(cb_[:, :], cosT[:, t0:t0 + T], eng="act")
        P.dma(sb_[:, :], sinT[:, t0:t0 + T], eng="act")
        _rms_xn(P, bank, x, xn, ones, sq, rstd, colv, T)
        ps = proj_fm(0, 128, T); P.act(zsil[:, :], ps[:, :T], AF.Silu)
        ps = proj_fm(128, 128, T); P.copy(xbuf[:, 3:3 + T], ps[:, :T], eng="act")
        ps = proj_fm(256, 64, T); P.copy(bbuf[:, 3:3 + T], ps[0:64, :T], eng="act")
        ps = proj_fm(320, 64, T); P.copy(cbuf[:, 3:3 + T], ps[0:64, :T], eng="act")
        _conv_silu(P, xc, xbuf, 128, T, colv, 8, 4, cacc)
        _conv_silu(P, Bc, bbuf, 64, T, colv, 13, 4, cacc)
        _conv_silu(P, Cc, cbuf, 64, T, colv, 18, 4, cacc)
        for h in range(2):
            for (dst, cq) in ((qrot[h], 384), (krot[h], 640)):
                p1 = proj_fm(cq + h * 64, 64, T)
                p2 = proj_fm(cq + 128 + h * 64, 64, T)
                P.tt(rt1[:, :], p1[0:64, :T], cb_[:, :], ALU.mult)
                P.tt(rt2[:, :], p2[0:64, :T], sb_[:, :], ALU.mult)
                P.tt(dst[:, :], rt1[:, :], rt2[:, :], ALU.add, eng="pool")
        ps = proj_fm(896, 128, T); P.act(gsil[:, :], ps[:, :T], AF.Silu)
        go_t = gout[sc % 2]
        ro_t = rout[sc % 2]
        if sc == 0:
            P.dbg("rstd", rstd[:, :], [128, TS]); P.dbg("zsil", zsil[:, :], [128, TS]); P.dbg("gsil", gsil[:, :], [128, TS])
            P.dbg("xc", xc[:, :], [128, TS]); P.dbg("Bc", Bc[:, :], [64, TS]); P.dbg("Cc", Cc[:, :], [64, TS])
            P.dbg("qrot0", qrot[0][:, :], [64, TS]); P.dbg("krot1", krot[1][:, :], [64, TS])
        for c in range(T // 128):
            cs = slice(c * 128, (c + 1) * 128)
            pv = bank()
            for k in range(8):
                P.mm(pv[:, 0:128], xn[:, k, cs], wtm_bf[:, k, 0:128], start=(k == 0), stop=(k == 7))
            P.copy(vtok[:, :], pv[:, 0:128], eng="act")
            pd = bank()
            for k in range(8):
                P.mm(pd[:, 0:2], xn[:, k, cs], wtm_bf[:, k, 128:130], start=(k == 0), stop=(k == 7))
            for h in range(2):
                P.act(ex[:, h:h + 1], pd[:, h:h + 1], AF.Exp, bias=rowb[:, h:h + 1])
            P.act(dt[:, :], ex[:, :], AF.Ln, bias=1.0)
            P.tt(a_[:, :], dt[:, :], nA[:, :], ALU.mult)
            pc = bank()
            P.mm(pc[:, 0:2], tri[:, :], a_[:, :])
            P.mm(pc[:, 2:4], ones[:, :], a_[:, :])
            P.ts(nacum[:, :], pc[:, 0:2], -1.0, ALU.mult)
            P.act(eac[:, :], pc[:, 0:2], AF.Exp)
            P.copy(tot[:, :], pc[:, 2:4])
            P.act(cd[:, :], pc[:, 2:4], AF.Exp)
            for h in range(2):
                P.act(toend[:, h:h + 1], pc[:, h:h + 1], AF.Exp, bias=tot[:, h:h + 1], scale=-1.0)
            px = bank()
            P.tr(px[:, 0:128], xc[:, cs], ident[:, :])
            for h in range(2):
                hs = slice(h * 64, (h + 1) * 64)
                P.ts(xdt[:, hs], px[:, hs], dt[:, h:h + 1], ALU.mult)
                P.ts(xte[:, hs], xdt[:, hs], toend[:, h:h + 1], ALU.mult, eng="pool")
            pbt = bank()
            P.tr(pbt[:, 0:64], Bc[0:64, cs], ident[0:64, 0:64])
            P.copy(btok[:, :], pbt[:, 0:64], eng="act")
            pbc = bank()
            P.mm(pbc[:, 0:128], Bc[0:64, cs], Cc[0:64, cs])
            for h in range(2):
                hs = slice(h * 64, (h + 1) * 64)
                P.ts(atri[h][:, :], tri[:, :], a_[:, h:h + 1], ALU.mult, eng="pool")
                psg = bank()
                P.mm(psg[:, 0:128], ones[:, :], atri[h][:, :], start=True, stop=False)
                P.mm(psg[:, 0:128], ident[:, :], maskb[:, :], start=False, stop=True)
                P.act(dT[h][:, :], psg[:, 0:128], AF.Exp, bias=nacum[:, h:h + 1])
                P.tt(mT[h][:, :], pbc[:, 0:128], dT[h][:, :], ALU.mult)
                pyd = bank()
                P.mm(pyd[:, 0:64], mT[h][:, :], xdt[:, hs])
                pyo = bank()
                P.mm(pyo[:, 0:64], Cc[0:64, cs], St[h][:, :])
                P.ts(tmpy[h][:, :], pyo[:, 0:64], eac[:, h:h + 1], ALU.mult)
                P.tt(Y[:, hs], pyd[:, 0:64], tmpy[h][:, :], ALU.add)
                pst = bank()
                P.mm(pst[0:64, 0:64], btok[:, :], xte[:, hs])
                P.stt(St[h][:, :], St[h][:, :], cd[0:64, h:h + 1], pst[0:64, 0:64], ALU.mult, ALU.add)
            pyt = bank()
            P.tr(pyt[:, 0:128], Y[:, :], ident[:, :])
            P.stt(go[:, :], xc[:, cs], colv[:, 23:24], pyt[:, 0:128], ALU.mult, ALU.add)
            P.tt(go_t[:, cs], go[:, :], zsil[:, cs], ALU.mult, eng="pool")
            for h in range(2):
                hs = slice(h * 64, (h + 1) * 64)
                pk = bank()
                P.tr(pk[:, 0:64], krot[h][:, cs], ident[0:64, 0:64])
                P.copy(ktok[h][:, :], pk[:, 0:64], eng="act")
                P.ts(vk[:, hs], vtok[:, hs], qkdec[:, 2 + h:3 + h], ALU.mult, eng="pool")
                pss = bank()
                P.mm(pss[:, 0:128], krot[h][:, cs], qrot[h][:, cs])
                P.tt(mTr[h][:, :], pss[:, 0:128], dtab[:, h, :], ALU.mult)
                pyi = bank()
                P.mm(pyi[:, 0:64], mTr[h][:, :], vtok[:, hs])
                pyx = bank()
                P.mm(pyx[:, 0:64], qrot[h][:, cs], R[h][:, :])
                P.ts(tmpy[h][:, :], pyx[:, 0:64], qkdec[:, h:h + 1], ALU.mult)
                P.tt(Yr[:, hs], pyi[:, 0:64], tmpy[h][:, :], ALU.add)
                P.act(junk[:, :], Yr[:, hs], AF.Square, accum_out=ssr[:, h:h + 1])
                pr = bank()
                P.mm(pr[0:64, 0:64], ktok[h][:, :], vk[:, hs])
                P.stt(R[h][:, :], R[h][:, :], colv[0:64, 25 + 0:26], pr[0:64, 0:64], ALU.mult, ALU.add) if h == 0 else \
                    P.stt(R[h][:, :], R[h][:, :], colv[0:64, 26:27], pr[0:64, 0:64], ALU.mult, ALU.add)
            P.act(rs[:, :], ssr[:, :], AF.Sqrt, bias=EPS, scale=1.0 / 64)
            P.recip(rs[:, :], rs[:, :])
            for h in range(2):
                hs = slice(h * 64, (h + 1) * 64)
                P.ts(Yn[:, hs], Yr[:, hs], rs[:, h:h + 1], ALU.mult)
            if sc == 0 and c == 0:
                for nm, tl, shp in (("vtok", vtok, [128, 128]), ("dt", dt, [128, 2]), ("a_", a_, [128, 2]), ("nacum", nacum, [128, 2]),
                                    ("eac", eac, [128, 2]), ("tot", tot, [128, 2]), ("cd", cd, [128, 2]), ("toend", toend, [128, 2]),
                                    ("xdt", xdt, [128, 128]), ("xte", xte, [128, 128]), ("btok", btok, [128, 64]), ("dT0", dT[0], [128, 128]),
                                    ("mT0", mT[0], [128, 128]), ("Y", Y, [128, 128]), ("St0", St[0], [64, 64]), ("go", go, [128, 128]),
                                    ("ktok0", ktok[0], [128, 64]), ("mTr0", mTr[0], [128, 128]), ("Yr", Yr, [128, 128]), ("ssr", ssr, [128, 2]),
                                    ("rs", rs, [128, 2]), ("Yn", Yn, [128, 128]), ("R0", R[0], [64, 64])):
                    P.dbg(nm, tl[tuple(slice(None) for _ in shp)], shp)
            pyr = bank()
            P.tr(pyr[:, 0:128], Yn[:, :], ident[:, :])
            P.stt(ro_t[:, cs], pyr[:, 0:128], colv[:, 24:25], gsil[:, cs], ALU.mult, ALU.mult)
        P.dma(yT[0:128, t0:t0 + T], go_t[:, :])
        P.dma(yT[128:256, t0:t0 + T], ro_t[:, :])
    P.finish([yT])
    P.emit()
    return nc, P


def _consts_common():
    idx = np.arange(128)
    ident = np.eye(128, dtype=np.float32)
    tri = (idx[:, None] <= idx[None, :]).astype(np.float32)
    maskb = np.where(idx[None, :] >= idx[:, None], 0.0, -1e9).astype(np.float32)
    return ident, tri, maskb


def _colT(v, n):
    return np.ascontiguousarray(np.asarray(v, np.float32).reshape(n, 128).T)


def prep_A0(inp, hp, S=SEQ):
    w_in = inp["ev_w_in"][0]
    g = hp // 2
    zc = np.arange(hp * 128, hp * 128 + 128)
    xc = 512 + np.arange(hp * 128, hp * 128 + 128)
    bc = 1024 + g * 64 + np.arange(64)
    cc = 1152 + g * 64 + np.arange(64)
    dtc = 1280 + 2 * hp + np.arange(2)
    rb = 1288
    loc = np.arange(hp * 128, hp * 128 + 128)
    sw = (loc // 64) * 64 + ((loc % 64) + 32) % 64
    qc, qs = rb + loc, rb + sw
    kc, ks = rb + 512 + loc, rb + 512 + sw
    vc = rb + 1024 + loc
    gc = rb + 1536 + loc
    wfm = np.ascontiguousarray(w_in[:, np.concatenate([zc, xc, bc, cc, qc, qs, kc, ks, gc])])
    wtm = np.ascontiguousarray(w_in[:, np.concatenate([vc, dtc])])
    colv = np.zeros((128, 27), np.float32)
    colv[:, 0:8] = _colT(inp["ev_norm_w"][0], 8)
    cw = inp["ev_ssd_conv_w"][0]; cb = inp["ev_ssd_conv_b"][0]
    xch = np.arange(hp * 128, hp * 128 + 128); bch = 512 + g * 64 + np.arange(64); cch = 640 + g * 64 + np.arange(64)
    for k in range(4):
        colv[:, 8 + k] = cw[k, xch]; colv[0:64, 13 + k] = cw[k, bch]; colv[0:64, 18 + k] = cw[k, cch]
    colv[:, 12] = cb[xch]; colv[0:64, 17] = cb[bch]; colv[0:64, 22] = cb[cch]
    colv[:, 23] = np.repeat(inp["ev_ssd_d"][0][2 * hp:2 * hp + 2], 64)
    colv[:, 24] = inp["ev_ret_norm_w"][0][loc]
    heads = 2 * hp + np.arange(2)
    lg = np.log1p(-(2.0 ** (-5.0 - heads.astype(np.float64))))
    colv[:, 25] = np.exp(128 * lg[0]); colv[:, 26] = np.exp(128 * lg[1])
    idx = np.arange(128, dtype=np.float64)
    rel = idx[None, :] - idx[:, None]
    dtab = np.zeros((128, 2, 128), np.float32)
    qkdec = np.zeros((128, 4), np.float32)
    for h in range(2):
        dtab[:, h, :] = np.where(rel >= 0, np.exp(np.maximum(rel, 0) * lg[h]), 0.0) * 0.125
        qkdec[:, h] = np.exp((idx + 1) * lg[h])
        qkdec[:, 2 + h] = np.exp((127 - idx) * lg[h]) * 0.125
    rowv = np.concatenate([inp["ev_ssd_dt_bias"][0][2 * hp:2 * hp + 2], inp["ev_ssd_a_log"][0][2 * hp:2 * hp + 2]]).astype(np.float32)[None, :]
    ident, tri, maskb = _consts_common()
    return dict(wfm=wfm, wtm=wtm, colv=colv, dtab=dtab, qkdec=qkdec, ident=ident, tri=tri, maskb=maskb, rowv=rowv)


def rot_tables(S=SEQ):
    half = 32
    inv = 1.0 / (10000.0 ** (np.arange(half, dtype=np.float32) / half))
    ang = np.arange(S, dtype=np.float32)[None, :] * inv[:, None]
    cos = np.cos(ang).astype(np.float32); sin = np.sin(ang).astype(np.float32)
    cosT = np.concatenate([cos, cos], 0)
    sinT = np.concatenate([-sin, sin], 0)
    return np.ascontiguousarray(cosT), np.ascontiguousarray(sinT)


RWKV_DECAY_SCALE = 0.606531
RWKV_LN_EPS = 64e-5


def build_phaseA1(S=SEQ, TS=512, debug=False):
    nc = bass.Bass("TRN2", target_bir_lowering=False)
    P = Prog(nc)
    P.debug = debug
    NSC = S // TS
    NBLK = S // 128
    rb = [P.ps(f"bank{i}", [128, 512]) for i in range(4)]
    accb = [P.ps(f"accb{i}", [128, 512]) for i in range(4)]
    rbi = [0]

    def bank():
        r = rb[rbi[0] % 4]
        rbi[0] += 1
        return r

    xT = P.dram("xT", [D, S], F32, kind="ExternalInput")
    wfm = P.dram("wfm", [D, 896], F32, kind="ExternalInput")
    wtm = P.dram("wtm", [D, 130], F32, kind="ExternalInput")
    rowv = P.dram("rowv", [1, 258], F32, kind="ExternalInput")
    yT = P.dram("yT", [256, S], F32, kind="ExternalOutput")
    C = _load_consts(P, [("colv", [128, 32]), ("lowr", [128, 384]), ("ident", [128, 128]), ("tri", [128, 128]),
                         ("msl", [128, 128]), ("msu", [128, 128]), ("maskq", [128, 128])])
    colv, lowr, ident, tri, msl, msu, maskq = (C[k] for k in ("colv", "lowr", "ident", "tri", "msl", "msu", "maskq"))
    rowb = P.sb("rowb", [128, 258])
    P.dma(rowb[:, :], V(rowv.h[0:1, :].partition_broadcast(128), rowv.trks, True))
    nfb = P.sb("nfb", [128, 2])
    P.ts(nfb[:, :], rowb[:, 256:258], -1.0, ALU.mult)
    ones = P.sb("ones", [128, 128]); P.memset(ones[:, :], 1.0)
    ones_bf = P.sb("ones_bf", [1, 128], BF16); P.memset(ones_bf[:, :], 1.0)
    ident_bf = P.sb("ident_bf", [128, 128], BF16); P.copy(ident_bf[:, :], ident[:, :])
    maskq_bf = P.sb("maskq_bf", [128, 128], BF16); P.copy(maskq_bf[:, :], maskq[:, :])
    qw8 = P.sb("qw8", [64, 1]); P.ts(qw8[:, :], colv[0:64, 30:31], 0.125, ALU.mult)

    wfm_bf = P.sb("wfm_bf", [128, 8, 896], BF16)
    wtm_bf = P.sb("wtm_bf", [128, 8, 130], BF16)
    stage = [P.sb(f"stage{i}", [128, 512], F32) for i in range(3)]
    _cast_weight(P, wfm_bf, lambda k, c0, cw: wfm[k * 128:(k + 1) * 128, c0:c0 + cw], 8, 896, stage)
    _cast_weight(P, wtm_bf, lambda k, c0, cw: wtm[k * 128:(k + 1) * 128, c0:c0 + cw], 8, 130, stage)

    xs_ = [P.sb(f"xs{i}", [128, 8, TS]) for i in range(1)]
    xn = P.sb("xn", [128, 8, TS], BF16)
    sq = [P.sb(f"sq{i}", [128, TS]) for i in range(2)]
    rstd = P.sb("rstd", [128, TS])
    shb = P.sb("shb", [128, TS + 1])
    hist = P.sb("hist", [128, 9]); P.memset(hist[:, :], 0.0)
    twlo = P.sb("twlo", [64, TS]); alos = P.sb("alos", [64, TS]); sglo = P.sb("sglo", [128, TS])
    dtmp = P.sb("dtmp", [128, TS])

    def hd(name, shape=None, dt_=F32):
        return [P.sb(f"{name}{h}", shape or [64, TS], dt_) for h in range(2)]
    def sh(name):
        t_ = P.sb(name, [64, TS])
        return [t_, t_]
    v_s, kmod, b_t, rkp = hd("v_s"), hd("kmod"), hd("b_t"), hd("rkp")
    logP, e1 = hd("logP"), hd("e1")
    rt, bt, kt, at = hd("rt"), hd("bt"), hd("kt"), hd("at")
    r_s, k_s, a_t, kkn, logw, e2, t1, t2 = sh("r_s"), sh("k_s"), sh("a_t"), sh("kkn"), sh("logw"), sh("e2"), sh("t1"), sh("t2")
    onesF = P.sb("onesF", [64, 128]); P.memset(onesF[:, :], 1.0)
    Sk = hd("Sk", [64, 64])
    for t_ in Sk:
        P.memset(t_[:, :], 0.0)

    def ch(name, shape=None, dt_=F32, n=2):
        return [P.sb(f"{name}{h}", shape or [128, 128], dt_) for h in range(n)]
    vtk, bdk, kdk = ch("vtk", [128, 64]), ch("bdk", [128, 64]), ch("kdk", [128, 64])
    decT, bdT, kdT = ch("decT", [64, 128]), ch("bdT", [64, 128]), ch("kdT", [64, 128])
    A_sb, AT_sb, AKT, RBT, RKT = ch("A_sb"), ch("AT_sb"), ch("AKT"), ch("RBT"), ch("RKT")
    Ma, Mb, MTa, MTb, Xa, Xb = ch("Ma"), ch("Mb"), ch("MTa"), ch("MTb"), ch("Xa"), ch("Xb")
    W_sb, U_sb = ch("W_sb", [128, 64]), ch("U_sb", [128, 64])
    bst, mv, rs_ = ch("bst", [128, 6]), ch("mv", [128, 2]), ch("rs_", [128, 1])
    yn = ch("yn", [128, 64])
    otok = [P.sb(f"otok{c}", [128, 128]) for c in range(TS // 128)]
    ro_t = [P.sb(f"ro_t{i}", [128, TS]) for i in range(1)]
    fo_t = [P.sb(f"fo_t{i}", [128, TS]) for i in range(1)]

    qraw, kraw = r_s, k_s
    qn = hd("qn", [64, TS], BF16)
    kcache = [T(P.es.enter_context(nc.sbuf_tensor(f"kcache{h}", [64, S], BF16)), f"kcache{h}", NSC) for h in range(2)]
    vcache = T(P.es.enter_context(nc.sbuf_tensor("vcache", [128, NBLK, 2, 65], BF16)), "vcache", NSC)
    negc = T(P.es.enter_context(nc.sbuf_tensor("negc", [128, NBLK, 2], F32)), "negc", NSC)
    P.memset(vcache[:, :, :, 64:65], 1.0)
    cqrow = [P.sb(f"cqrow{h}", [1, TS], BF16) for h in range(2)]
    carry = P.sb("carry", [128, 2]); P.memset(carry[:, :], 0.0)
    fe = P.sb("fe", [128, 2]); logf = P.sb("logf", [128, 2])
    pT = [P.sb(f"pT{i}", [128, TS], BF16) for i in range(2)]
    ftok = [P.sb(f"ftok{c}", [128, 128]) for c in range(TS // 128)]
    rden = P.sb("rden", [128, 1])

    xT3 = xT.h.rearrange("(k p) t -> p k t", p=128)

    def proj_fm(c0, M, T_):
        ps = bank()
        for k in range(8):
            P.mm(ps[0:M, :T_], wfm_bf[:, k, c0:c0 + M], xn[:, k, :T_], start=(k == 0), stop=(k == 7))
        return ps

    def shift(dst, ps_ap, rows, mucol, hcol, T_):
        r = slice(0, rows)
        P.copy(shb[r, 0:1], hist[r, hcol:hcol + 1], eng="pool")
        P.copy(shb[r, 1:1 + T_], ps_ap, eng="act")
        P.copy(hist[r, hcol:hcol + 1], shb[r, T_:T_ + 1], eng="pool")
        P.tt(dtmp[r, :T_], shb[r, 0:T_], shb[r, 1:1 + T_], ALU.subtract)
        P.stt(dst[r, :T_], dtmp[r, :T_], colv[r, mucol:mucol + 1], shb[r, 1:1 + T_], ALU.mult, ALU.add)

    for sc in range(NSC):
        t0 = sc * TS
        T_ = TS
        NCH = T_ // 128
        x = xs_[0]
        P.dma(x[:, :, :], V(xT3[:, :, t0:t0 + T_], xT.trks, True))
        _rms_xn(P, bank, x, xn, ones, sq, rstd, colv, T_)
        ps = proj_fm(384, 64, T_)
        shift(twlo, ps[0:64, :T_], 64, 11, 0, T_); P.act(twlo[:, :], twlo[:, :], AF.Tanh)
        ps = proj_fm(448, 64, T_)
        shift(alos, ps[0:64, :T_], 64, 12, 1, T_)
        ps = proj_fm(512, 128, T_)
        shift(sglo, ps[:, :T_], 128, 13, 2, T_); P.act(sglo[:, :], sglo[:, :], AF.Sigmoid)
        for h in range(2):
            for (raw, c0, wcol, dst) in ((qraw[h], 640 + h * 64, None, None), (kraw[h], 768 + h * 64, None, None)):
                ps = proj_fm(c0, 64, T_)
                P.copy(raw[:, :], ps[0:64, :T_], eng="act")
            for (raw, wc, is_q) in ((qraw[h], qw8[:, 0:1], True), (kraw[h], colv[0:64, 31:32], False)):
                s_ = sq[h]
                P.act(s_[0:64, :], raw[:, :], AF.Square)
                pss = bank()
                P.mm(pss[0:64, :T_], ones[0:64, 0:64], s_[0:64, :])
                P.act(dtmp[0:64, :T_], pss[0:64, :T_], AF.Sqrt, bias=EPS, scale=1.0 / 64)
                P.recip(dtmp[0:64, :T_], dtmp[0:64, :T_])
                if is_q:
                    P.stt(qn[h][:, :], raw[:, :], wc, dtmp[0:64, :T_], ALU.mult, ALU.mult)
                else:
                    P.stt(kcache[h].v((slice(None), slice(t0, t0 + T_)), [sc]), raw[:, :], wc, dtmp[0:64, :T_], ALU.mult, ALU.mult)
        for c in range(NCH):
            cs = slice(c * 128, (c + 1) * 128)
            blk = sc * NCH + c
            pv = bank()
            for k in range(8):
                P.mm(pv[:, 0:130], xn[:, k, cs], wtm_bf[:, k, 0:130], start=(k == 0), stop=(k == 7))
            for h in range(2):
                P.copy(vcache.v((slice(None), blk, h, slice(0, 64)), [sc]), pv[:, h * 64:(h + 1) * 64], eng="act")
                P.act(fe[:, h:h + 1], pv[:, 128 + h:129 + h], AF.Exp, bias=nfb[:, h:h + 1], scale=-1.0)
            P.act(logf[:, :], fe[:, :], AF.Ln, bias=1.0)
            P.ts(logf[:, :], logf[:, :], -1.0, ALU.mult)
            pc = bank()
            P.mm(pc[:, 0:2], tri[:, :], logf[:, :])
            P.mm(pc[:, 2:4], ones[:, :], logf[:, :])
            P.stt(negc.v((slice(None), blk, slice(None)), [sc]), pc[:, 0:2], -1.0, carry[:, :], ALU.mult, ALU.subtract)
            for h in range(2):
                pr = bank()
                P.mm(pr[0:1, 0:128], logf[:, h:h + 1], tri[:, :])
                P.ts(cqrow[h][0:1, cs], pr[0:1, 0:128], carry[0:1, h:h + 1], ALU.add)
            P.tt(carry[:, :], carry[:, :], pc[:, 2:4], ALU.add)
        pg_list = []
        for h in range(2):
            hc = slice(0, 64)
            for (hcol, c0, mucol, dst) in ((3 + h, 0 + h * 64, 14 + h, r_s[h]), (5 + h, 128 + h * 64, 16 + h, k_s[h]),
                                           (7 + h, 256 + h * 64, 18 + h, v_s[h])):
                ps = proj_fm(c0, 64, T_)
                shift(dst, ps[0:64, :T_], 64, mucol, hcol, T_)
            ps = bank()
            P.mm(ps[0:64, :T_], lowr[0:64, 128 + h * 64:128 + (h + 1) * 64], twlo[:, :])
            P.act(logw[h][:, :], ps[0:64, :T_], AF.Sigmoid, bias=colv[0:64, 20 + h:21 + h])
            P.ts(logw[h][:, :], logw[h][:, :], -RWKV_DECAY_SCALE, ALU.mult, eng="pool")
            ps = bank()
            P.mm(ps[0:64, :T_], lowr[0:64, 256 + h * 64:256 + (h + 1) * 64], alos[:, :])
            P.act(a_t[h][:, :], ps[0:64, :T_], AF.Sigmoid, bias=colv[0:64, 22 + h:23 + h])
            P.ts(t1[h][:, :], k_s[h][:, :], colv[0:64, 24 + h:25 + h], ALU.mult)
            P.act(t2[h][:, :], t1[h][:, :], AF.Square)
            ps = bank()
            P.mm(ps[0:64, :T_], ones[0:64, 0:64], t2[h][:, :])
            P.act(t2[h][:, :], ps[0:64, :T_], AF.Sqrt)
            P.ts(t2[h][:, :], t2[h][:, :], 1e-12, ALU.max)
            P.recip(t2[h][:, :], t2[h][:, :])
            P.tt(kkn[h][:, :], t1[h][:, :], t2[h][:, :], ALU.mult)
            P.ts(t1[h][:, :], a_t[h][:, :], colv[0:64, 26 + h:27 + h], ALU.mult, colv[0:64, 28 + h:29 + h], ALU.add)
            P.tt(kmod[h][:, :], k_s[h][:, :], t1[h][:, :], ALU.mult)
            P.tt(b_t[h][:, :], kkn[h][:, :], a_t[h][:, :], ALU.mult, eng="pool")
            P.stt(rkp[h][:, :], r_s[h][:, :], colv[0:64, 9 + h:10 + h], kmod[h][:, :], ALU.mult, ALU.mult)
            for c in range(NCH):
                cs = slice(c * 128, (c + 1) * 128)
                P.op("dve", (lambda e, o=logP[h].h[:, cs], d1=logw[h].h[:, cs]: e.tensor_tensor_scan(o, onesF.h[:, :], d1, 0.0, ALU.mult, ALU.add)),
                     reads=[logw[h][:, cs], onesF[:, :]], writes=[logP[h][:, cs]])
            P.act(e1[h][:, :], logP[h][:, :], AF.Exp)
            P.act(e2[h][:, :], logP[h][:, :], AF.Exp, scale=-1.0)
            P.tt(rt[h][:, :], r_s[h][:, :], e1[h][:, :], ALU.mult)
            P.tt(bt[h][:, :], b_t[h][:, :], e2[h][:, :], ALU.mult, eng="pool")
            P.tt(kt[h][:, :], kmod[h][:, :], e2[h][:, :], ALU.mult)
            P.tt(t1[h][:, :], logP[h][:, :], logw[h][:, :], ALU.subtract, eng="pool")
            P.act(t1[h][:, :], t1[h][:, :], AF.Exp)
            P.stt(at[h][:, :], kkn[h][:, :], -1.0, t1[h][:, :], ALU.mult, ALU.mult)
            if sc == 0 and h == 0:
                for nm, tl in (("r_s", r_s), ("k_s", k_s), ("v_s", v_s), ("a_t", a_t), ("kkn", kkn), ("kmod", kmod), ("logw", logw),
                               ("logP", logP), ("rt", rt), ("bt", bt), ("kt", kt), ("at", at), ("rkp", rkp)):
                    P.dbg(nm, tl[h][:, 0:128], [64, 128])
        for c in range(NCH):
            cs = slice(c * 128, (c + 1) * 128)
            last = c * 128 + 127
            for h in range(2):
                hs = slice(h * 64, (h + 1) * 64)
                P.act(decT[h][:, :], logP[h][:, cs], AF.Exp, bias=logP[h][:, last:last + 1], scale=-1.0)
                P.tt(bdT[h][:, :], b_t[h][:, cs], decT[h][:, :], ALU.mult, eng="pool")
                P.tt(kdT[h][:, :], kmod[h][:, cs], decT[h][:, :], ALU.mult)
                for (src, dst) in ((v_s[h][:, cs], vtk[h]), (bdT[h][:, :], bdk[h]), (kdT[h][:, :], kdk[h])):
                    pt = bank()
                    P.tr(pt[:, 0:64], src, ident[0:64, 0:64])
                    P.copy(dst[:, :], pt[:, 0:64], eng="act")
                atc, btc, ktc, rtc = at[h][:, cs], bt[h][:, cs], kt[h][:, cs], rt[h][:, cs]
                for (l_, r_, msk, dst) in ((atc, btc, msl, A_sb[h]), (btc, atc, msu, AT_sb[h]), (ktc, atc, msu, AKT[h]),
                                           (btc, rtc, tri, RBT[h]), (ktc, rtc, tri, RKT[h])):
                    pm = bank()
                    P.mm(pm[:, 0:128], l_, r_)
                    P.tt(dst[:, :], pm[:, 0:128], msk[:, :], ALU.mult)
                pw = bank()
                P.mm(pw[:, 0:64], atc, Sk[h][:, :], start=True, stop=False)
                P.mm(pw[:, 0:64], AKT[h][:, :], vtk[h][:, :], start=False, stop=True)
                P.copy(W_sb[h][:, :], pw[:, 0:64], eng="act")
                M, MT = A_sb[h], AT_sb[h]
                X = Xa[h]
                P.tt(X[:, :], ident[:, :], AT_sb[h][:, :], ALU.add, eng="pool")
                Ms, MTs, Xs = (Ma[h], Mb[h]), (MTa[h], MTb[h]), (Xb[h], Xa[h])
                for it in range(6):
                    p2 = bank()
                    P.mm(p2[:, 0:128], MT[:, :], M[:, :])
                    Mn = Ms[it % 2]
                    P.copy(Mn[:, :], p2[:, 0:128], eng="act")
                    if it < 5:
                        p3 = bank()
                        P.mm(p3[:, 0:128], M[:, :], MT[:, :])
                        MTn = MTs[it % 2]
                        P.copy(MTn[:, :], p3[:, 0:128])
                    px = bank()
                    P.mm(px[:, 0:128], Mn[:, :], X[:, :])
                    Xn = Xs[it % 2]
                    P.tt(Xn[:, :], X[:, :], px[:, 0:128], ALU.add)
                    M, X = Mn, Xn
                    if it < 5:
                        MT = MTn
                pu = bank()
                P.mm(pu[:, 0:64], X[:, :], W_sb[h][:, :])
                P.copy(U_sb[h][:, :], pu[:, 0:64], eng="act")
                if sc == 0 and h == 0 and c == 0:
                    P.dbg("A", A_sb[h][:, :], [128, 128]); P.dbg("AT", AT_sb[h][:, :], [128, 128]); P.dbg("X", X[:, :], [128, 128])
                    P.dbg("W", W_sb[h][:, :], [128, 64]); P.dbg("U", U_sb[h][:, :], [128, 64]); P.dbg("vtk", vtk[h][:, :], [128, 64])
                py = bank()
                P.mm(py[:, 0:64], rtc, Sk[h][:, :], start=True, stop=False)
                P.mm(py[:, 0:64], RBT[h][:, :], U_sb[h][:, :], start=False, stop=False)
                P.mm(py[:, 0:64], RKT[h][:, :], vtk[h][:, :], start=False, stop=True)
                P.mm(py[:, 64:65], rkp[h][:, cs], ones[0:64, 0:1], start=True, stop=True)
                pS = bank()
                P.mm(pS[0:64, 0:64], bdk[h][:, :], U_sb[h][:, :], start=True, stop=False)
                P.mm(pS[0:64, 0:64], kdk[h][:, :], vtk[h][:, :], start=False, stop=True)
                P.stt(Sk[h][:, :], Sk[h][:, :], e1[h][:, last:last + 1], pS[0:64, 0:64], ALU.mult, ALU.add)
                P.op("dve", (lambda e, o=bst[h].h[:, :], i_=py.h[:, 0:64]: e.bn_stats(o, i_)), reads=[py[:, 0:64]], writes=[bst[h][:, :]])
                P.op("dve", (lambda e, o=mv[h].h[:, :], i_=bst[h].h[:, :]: e.bn_aggr(o, i_)), reads=[bst[h][:, :]], writes=[mv[h][:, :]])
                P.act(rs_[h][:, :], mv[h][:, 1:2], AF.Sqrt, bias=RWKV_LN_EPS)
                P.recip(rs_[h][:, :], rs_[h][:, :])
                P.ts(yn[h][:, :], py[:, 0:64], mv[h][:, 0:1], ALU.subtract, rs_[h][:, 0:1], ALU.mult)
                P.tt(yn[h][:, :], yn[h][:, :], rowb[:, hs], ALU.mult, eng="pool")
                P.tt(yn[h][:, :], yn[h][:, :], rowb[:, 128 + h * 64:128 + (h + 1) * 64], ALU.add, eng="pool")
                P.stt(yn[h][:, :], vtk[h][:, :], py[:, 64:65], yn[h][:, :], ALU.mult, ALU.add)
                if sc == 0 and h == 0 and c == 0:
                    P.dbg("yn", yn[h][:, :], [128, 64]); P.dbg("mv", mv[h][:, :], [128, 2])
                pg = bank()
                P.mm(pg[:, 0:64], sglo[:, cs], lowr[:, h * 64:(h + 1) * 64])
                P.tt(otok[c][:, hs], yn[h][:, :], pg[:, 0:64], ALU.mult)
            pt = bank()
            P.tr(pt[:, 0:128], otok[c][:, :], ident[:, :])
            P.copy(ro_t[0][:, cs], pt[:, 0:128], eng="act")
        P.dma(yT[0:128, t0:t0 + T_], ro_t[0][:, :])
        nkb = (sc + 1) * NCH
        for h in range(2):
            for kb in range(nkb):
                j = kb - sc * NCH
                q0 = max(j, 0) * 128
                ksc = kb // NCH
                ps = bank()
                P.mm(ps[:, q0:T_], kcache[h].v((slice(None), slice(kb * 128, (kb + 1) * 128)), [ksc]), qn[h][:, q0:T_],
                     start=True, stop=False)
                P.mm(ps[:, q0:T_], ones_bf[0:1, :], cqrow[h][0:1, q0:T_], start=False, stop=(j < 0))
                if j >= 0:
                    P.mm(ps[:, q0:q0 + 128], ident_bf[:, :], maskq_bf[:, :], start=False, stop=True)
                p_ = pT[kb % 2]
                P.act(p_[:, q0:T_], ps[:, q0:T_], AF.Exp, bias=negc.v((slice(None), kb, slice(h, h + 1)), [ksc]))
                for jq in range(max(j, 0), NCH):
                    P.mm(accb[jq][:, 0:65], p_[:, jq * 128:(jq + 1) * 128],
                         vcache.v((slice(None), kb, h, slice(0, 65)), [ksc]),
                         start=(kb == 0), stop=(kb == sc * NCH + jq))
            for jq in range(NCH):
                P.recip(rden[:, :], accb[jq][:, 64:65])
                P.ts(ftok[jq][:, h * 64:(h + 1) * 64], accb[jq][:, 0:64], rden[:, 0:1], ALU.mult)
        for jq in range(NCH):
            pt = bank()
            P.tr(pt[:, 0:128], ftok[jq][:, :], ident[:, :])
            P.copy(fo_t[0][:, jq * 128:(jq + 1) * 128], pt[:, 0:128], eng="act")
        P.dma(yT[128:256, t0:t0 + T_], fo_t[0][:, :])
    P.finish([yT])
    P.emit()
    return nc, P


def prep_A1(inp, hp):
    w_in = inp["od_w_in"][0]
    loc = np.arange(hp * 128, hp * 128 + 128)
    fb = 1792
    cols = np.concatenate([loc, 512 + loc, 1024 + loc, 1536 + np.arange(64), 1600 + np.arange(64), 1664 + np.arange(128),
                           fb + loc, fb + 512 + loc])
    wfm = np.ascontiguousarray(w_in[:, cols])
    wtm = np.ascontiguousarray(w_in[:, np.concatenate([fb + 1024 + loc, fb + 1536 + 2 * hp + np.arange(2)])])
    mu = inp["od_rwkv_mu"][0]
    colv = np.zeros((128, 32), np.float32)
    colv[:, 0:8] = _colT(inp["od_norm_w"][0], 8)
    g = lambda v, h: np.asarray(v, np.float32).reshape(-1)[(2 * hp + h) * 64:(2 * hp + h + 1) * 64]
    for h in range(2):
        colv[0:64, 9 + h] = g(inp["od_rwkv_r_k"][0], h)
        colv[0:64, 14 + h] = mu[0 + (2 * hp + h) * 64:][:64]
        colv[0:64, 16 + h] = mu[512 + (2 * hp + h) * 64:][:64]
        colv[0:64, 18 + h] = mu[1024 + (2 * hp + h) * 64:][:64]
        colv[0:64, 20 + h] = g(inp["od_rwkv_w0"][0], h)
        colv[0:64, 22 + h] = g(inp["od_rwkv_a0"][0], h)
        colv[0:64, 24 + h] = g(inp["od_rwkv_k_k"][0], h)
        ka = g(inp["od_rwkv_k_a"][0], h)
        colv[0:64, 26 + h] = ka
        colv[0:64, 28 + h] = 1.0 - ka
    colv[0:64, 11] = mu[1536:1600]; colv[0:64, 12] = mu[1600:1664]; colv[:, 13] = mu[1664:1792]
    colv[0:64, 30] = inp["od_fox_q_norm_w"][0]; colv[0:64, 31] = inp["od_fox_k_norm_w"][0]
    lowr = np.zeros((128, 384), np.float32)
    lowr[:, 0:128] = inp["od_rwkv_g_up"][0][:, loc]
    lowr[0:64, 128:256] = inp["od_rwkv_w_up"][0][:, loc]
    lowr[0:64, 256:384] = inp["od_rwkv_a_up"][0][:, loc]
    rowv = np.concatenate([inp["od_rwkv_ln_w"][0][loc], inp["od_rwkv_ln_b"][0][loc],
                           inp["od_fox_f_bias"][0][2 * hp:2 * hp + 2]]).astype(np.float32)[None, :]
    ident, tri, _ = _consts_common()
    idx = np.arange(128)
    msl = (idx[None, :] < idx[:, None]).astype(np.float32)
    msu = (idx[None, :] > idx[:, None]).astype(np.float32)
    maskq = np.where(idx[None, :] >= idx[:, None], 0.0, -30000.0).astype(np.float32)
    return dict(wfm=wfm, wtm=wtm, colv=colv, lowr=lowr, rowv=rowv, ident=ident, tri=tri, msl=msl, msu=msu, maskq=maskq)


def prep_B(inp, layer):
    even = (layer % 2 == 0)
    i = layer // 2
    pre = "ev" if even else "od"
    vec = np.zeros((128, 12), np.float32)
    vec[:, 0:8] = _colT(inp["ffn_norm_w"][layer], 8)
    if even:
        vec[:, 8:12] = _colT(inp["ev_ssd_norm_w"][i], 4)
    cv = np.zeros((128, NFC, 4), np.float32)
    for k in range(3):
        cv[:, :, k] = _colT(inp["ffn_conv_w"][layer][k], NFC)
    cv[:, :, 3] = _colT(inp["ffn_conv_b"][layer], NFC)
    return dict(w_out=np.ascontiguousarray(inp[pre + "_w_out"][i], np.float32),
                w_up=np.ascontiguousarray(inp["ffn_w_up"][layer], np.float32),
                w_down=np.ascontiguousarray(inp["ffn_w_down"][layer], np.float32),
                vec_d=vec, cv_d=np.ascontiguousarray(cv.reshape(128, NFC * 4)))


def _halo_slice(aT, q, NT):
    out = np.zeros((aT.shape[0], NT + 2), np.float32)
    lo = q * NT - 2
    if lo < 0:
        out[:, 2:] = aT[:, 0:NT]
    else:
        out[:] = aT[:, lo:lo + NT + 2]
    return out


_PROGS = {}


def _prog(name, fn):
    if name not in _PROGS:
        _PROGS[name] = fn()[0]
    return _PROGS[name]


def _dev_run(nc, maps):
    from concourse.bass_utils import run_bass_kernel_spmd
    return run_bass_kernel_spmd(nc, maps, core_ids=list(range(len(maps)))).results


def _run_A(run, nc, inp, xT, prep, extra, NB, S):
    maps = []
    for core in range(NB * 4):
        b, hp = core // 4, core % 4
        m = prep(inp, hp)
        m["xT"] = xT[b]
        m.update(extra)
        maps.append(m)
    res = run(nc, maps)
    yT = [np.zeros((D, S), np.float32) for _ in range(NB)]
    for core in range(NB * 4):
        b, hp = core // 4, core % 4
        o = res[core]["yT"]
        yT[b][hp * 128:(hp + 1) * 128] = o[0:128]
        yT[b][512 + hp * 128:512 + (hp + 1) * 128] = o[128:256]
    return yT


def _run_B(run, nc, inp, layer, xT, yT, NB, S):
    NT = S // 4
    w = prep_B(inp, layer)
    maps = []
    for core in range(NB * 4):
        b, q = core // 4, core % 4
        m = dict(w)
        m["xT"] = _halo_slice(xT[b], q, NT)
        m["yT"] = _halo_slice(yT[b], q, NT)
        maps.append(m)
    res = run(nc, maps)
    out = [np.zeros((D, S), np.float32) for _ in range(NB)]
    for core in range(NB * 4):
        b, q = core // 4, core % 4
        out[b][:, q * NT:(q + 1) * NT] = res[core]["oT"]
    return out


def pipeline(inp, run, NB=2, S=SEQ):
    x = inp["x"].astype(np.float32)
    xT = [np.ascontiguousarray(x[b, :S].T) for b in range(NB)]
    cosT, sinT = rot_tables(S)
    y0 = _run_A(run, _prog(("A0", S), lambda: build_phaseA0(S=S)), inp, xT, prep_A0, dict(cosT=cosT, sinT=sinT), NB, S)
    x2 = _run_B(run, _prog(("B0", S), lambda: build_phaseB(True, NT=S // 4)), inp, 0, xT, y0, NB, S)
    y1 = _run_A(run, _prog(("A1", S), lambda: build_phaseA1(S=S)), inp, x2, prep_A1, {}, NB, S)
    x4 = _run_B(run, _prog(("B1", S), lambda: build_phaseB(False, NT=S // 4)), inp, 1, x2, y1, NB, S)
    return np.ascontiguousarray(np.stack([x4[b].T for b in range(NB)], 0)).astype(np.float32)


def kernel(**inputs):
    inp = {k: np.asarray(v) for k, v in inputs.items()}
    return pipeline(inp, _dev_run)
```
